# Optimizing a Trainium2 kernel written in Bass

```python
import jax, jax.numpy as jnp
from jax import lax
import numpy as np

D_MODEL = 2048
BATCH = 2
SEQ = 4096
DEPTH = 4

MIX_A = D_MODEL // 4
MIX_B = D_MODEL // 2
MIX_C = D_MODEL // 4
GM_GROUPS = 4
GM_GROUP_W = MIX_A // GM_GROUPS
CHUNK = 128
RWKV_HEAD = 64
RWKV_HEADS = MIX_B // RWKV_HEAD
DECAY_LORA = 96
ICLR_LORA = 96
GATE_LORA = 256
RWKV_COLS = 3 * MIX_B + DECAY_LORA + ICLR_LORA + GATE_LORA
CONV_K = 3
CONV_GROUPS = 8
CONV_GROUP_W = MIX_C // CONV_GROUPS
N_IN = 2 * MIX_A + RWKV_COLS + 3 * MIX_C
D_FF = 4 * D_MODEL
RMS_EPS = 1e-6
LNX_EPS = 64e-5

kernel_name = "hybrid_gmlp_rwkv7_shortconv_trunk"


def rms_norm(x, gain):
    xf = x.astype(jnp.float32)
    y = xf * lax.rsqrt(jnp.mean(jnp.square(xf), axis=-1, keepdims=True) + RMS_EPS)
    return (y * gain.astype(jnp.float32)).astype(x.dtype)


def gmlp_mix(p, v_gain, ws, bs):
    b_, t_, _ = p.shape
    z = jax.nn.gelu(p)
    u, v = jnp.split(z, 2, axis=-1)
    u = u.reshape(b_, t_, GM_GROUPS, GM_GROUP_W)
    v = rms_norm(v.reshape(b_, t_, GM_GROUPS, GM_GROUP_W), v_gain.reshape(GM_GROUPS, GM_GROUP_W))
    vc = v.reshape(b_, t_ // CHUNK, CHUNK, GM_GROUPS, GM_GROUP_W)
    causal = jnp.tril(jnp.ones((CHUNK, CHUNK), dtype=bool))
    w = jnp.where(causal[None], ws, 0)
    mixed = jnp.einsum('gts,bcsgd->bctgd', w, vc) + bs.T[None, None, :, :, None]
    return u * mixed.reshape(b_, t_, GM_GROUPS, GM_GROUP_W)


def token_shift(p, mu):
    prev = jnp.pad(p, ((0, 0), (1, 0), (0, 0)))[:, :-1]
    return p + (prev - p) * mu


def wkv7_scan(r, w, k, v, a, b):
    b_, t_, h_, n_ = r.shape
    xs = tuple(jnp.moveaxis(t.astype(jnp.float32), 1, 0) for t in (r, w, k, v, a, b))

    def step(S, inp):
        r_t, w_t, k_t, v_t, a_t, b_t = inp
        sa = jnp.einsum('bhij,bhj->bhi', S, a_t)
        S = S * w_t[:, :, None, :] + sa[..., :, None] * b_t[..., None, :] + v_t[..., :, None] * k_t[..., None, :]
        y = jnp.einsum('bhij,bhj->bhi', S, r_t)
        return S, y

    S0 = jnp.zeros((b_, h_, n_, n_), jnp.float32)
    _, ys = lax.scan(step, S0, xs)
    return jnp.moveaxis(ys, 0, 1)


def rwkv7_mix(p, mu, w0, w_up, a0, a_up, g_up, k_k, k_a, r_k, ln_w, ln_b):
    b_, t_, _ = p.shape
    p = token_shift(p, mu)
    idx = [MIX_B, 2 * MIX_B, 3 * MIX_B, 3 * MIX_B + DECAY_LORA, 3 * MIX_B + DECAY_LORA + ICLR_LORA]
    r, k, v, wd, ad, gd = jnp.split(p, idx, axis=-1)
    w = -jax.nn.softplus(-(w0 + jnp.tanh(wd) @ w_up)) - 0.5
    decay = jnp.exp(-jnp.exp(w.astype(jnp.float32)))
    a = jax.nn.sigmoid(a0 + ad @ a_up)
    g = jax.nn.sigmoid(gd) @ g_up
    heads = lambda t: t.reshape(b_, t_, RWKV_HEADS, RWKV_HEAD)
    kkf = heads(k * k_k).astype(jnp.float32)
    kk = kkf / jnp.maximum(jnp.sqrt(jnp.sum(jnp.square(kkf), axis=-1, keepdims=True)), 1e-12)
    k = k * (1 + (a - 1) * k_a)
    rh, kh, vh, ah = heads(r), heads(k), heads(v), heads(a)
    y = wkv7_scan(rh, heads(decay), kh, vh, -kk, kk * ah.astype(jnp.float32))
    mean = jnp.mean(y, axis=-1, keepdims=True)
    var = jnp.mean(jnp.square(y - mean), axis=-1, keepdims=True)
    y = (y - mean) * lax.rsqrt(var + LNX_EPS)
    y = (y * ln_w.reshape(RWKV_HEADS, RWKV_HEAD) + ln_b.reshape(RWKV_HEADS, RWKV_HEAD)).astype(p.dtype)
    bonus = jnp.sum(rh * kh * r_k, axis=-1, keepdims=True) * vh
    return ((y + bonus) * heads(g)).reshape(b_, t_, MIX_B)


def short_conv_mix(p, conv_w):
    gb, gc, h = jnp.split(p, 3, axis=-1)
    z = gc * h
    zc = lax.conv_general_dilated(z, conv_w[:, None, :], window_strides=(1,), padding=[(CONV_K - 1, 0)],
                                  dimension_numbers=('NWC', 'WIO', 'NWC'), feature_group_count=MIX_C)
    return gb * zc


def setup_inputs(seed: int = 0) -> dict:
    key = jax.random.key(seed)
    ks = iter(jax.random.split(key, 40))
    L = DEPTH
    f32 = jnp.float32

    def normal(shape, scale):
        return jax.random.normal(next(ks), shape, f32) * scale

    def gain(shape):
        return 1.0 + normal(shape, 0.02)

    def unif(shape, lo, hi):
        return jax.random.uniform(next(ks), shape, f32, lo, hi)

    return {
        "x": normal((BATCH, SEQ, D_MODEL), 1.0),
        "norm_mix_pre": gain((L, D_MODEL)),
        "norm_mix_post": gain((L, D_MODEL)),
        "norm_mlp_pre": gain((L, D_MODEL)),
        "norm_mlp_post": gain((L, D_MODEL)),
        "w_in": normal((L, D_MODEL, N_IN), D_MODEL ** -0.5),
        "gm_v_gain": gain((L, MIX_A)),
        "gm_ws": normal((L, GM_GROUPS, CHUNK, CHUNK), CHUNK ** -0.5),
        "gm_bs": gain((L, GM_GROUPS, CHUNK)),
        "gm_out_gain": gain((L, MIX_A)),
        "rk_mu": unif((L, RWKV_COLS), 0.0, 1.0),
        "rk_w0": unif((L, MIX_B), -5.0, -1.0),
        "rk_w_up": normal((L, DECAY_LORA, MIX_B), 0.5 * DECAY_LORA ** -0.5),
        "rk_a0": normal((L, MIX_B), 0.1),
        "rk_a_up": normal((L, ICLR_LORA, MIX_B), ICLR_LORA ** -0.5),
        "rk_g_up": normal((L, GATE_LORA, MIX_B), GATE_LORA ** -0.5),
        "rk_k_k": 1.0 + normal((L, MIX_B), 0.1),
        "rk_k_a": 1.0 + normal((L, MIX_B), 0.1),
        "rk_r_k": normal((L, RWKV_HEADS, RWKV_HEAD), 0.1),
        "rk_ln_w": gain((L, MIX_B)),
        "rk_ln_b": normal((L, MIX_B), 0.02),
        "sc_conv": normal((L, CONV_K, MIX_C), CONV_K ** -0.5),
        "sc_out_gain": gain((L, MIX_C)),
        "w_out": normal((L, D_MODEL, D_MODEL), D_MODEL ** -0.5),
        "mlp_up": normal((L, D_MODEL, D_FF), D_MODEL ** -0.5),
        "mlp_down": normal((L, D_FF, D_MODEL), D_FF ** -0.5),
    }


def reference(x, norm_mix_pre, norm_mix_post, norm_mlp_pre, norm_mlp_post, w_in, gm_v_gain, gm_ws, gm_bs,
              gm_out_gain, rk_mu, rk_w0, rk_w_up, rk_a0, rk_a_up, rk_g_up, rk_k_k, rk_k_a, rk_r_k, rk_ln_w,
              rk_ln_b, sc_conv, sc_out_gain, w_out, mlp_up, mlp_down):
    b_, t_, _ = x.shape
    for l in range(DEPTH):
        h = rms_norm(x, norm_mix_pre[l])
        p = h @ w_in[l]
        pa, pb, pc = jnp.split(p, [2 * MIX_A, 2 * MIX_A + RWKV_COLS], axis=-1)
        ya = rms_norm(gmlp_mix(pa, gm_v_gain[l], gm_ws[l], gm_bs[l]),
                      gm_out_gain[l].reshape(GM_GROUPS, GM_GROUP_W)).reshape(b_, t_, MIX_A)
        yb = rwkv7_mix(pb, rk_mu[l], rk_w0[l], rk_w_up[l], rk_a0[l], rk_a_up[l], rk_g_up[l], rk_k_k[l],
                       rk_k_a[l], rk_r_k[l], rk_ln_w[l], rk_ln_b[l])
        yc = rms_norm(short_conv_mix(pc, sc_conv[l]).reshape(b_, t_, CONV_GROUPS, CONV_GROUP_W),
                      sc_out_gain[l].reshape(CONV_GROUPS, CONV_GROUP_W)).reshape(b_, t_, MIX_C)
        y = jnp.concatenate([ya, yb, yc], axis=-1) @ w_out[l]
        x = x + rms_norm(y, norm_mix_post[l])
        h = rms_norm(x, norm_mlp_pre[l])
        f = jnp.square(jax.nn.relu(h @ mlp_up[l])) @ mlp_down[l]
        x = x + rms_norm(f, norm_mlp_post[l])
    return x
```

```python
import numpy as np
import concourse.bass as bass
import concourse.mybir as mybir
from concourse.bass_utils import run_bass_kernel_spmd
from contextlib import ExitStack

F32 = mybir.dt.float32
F32R = mybir.dt.float32r
AF = mybir.ActivationFunctionType
ALU = mybir.AluOpType
AX = mybir.AxisListType

NDMA_SEMS = 12


class Prog:
    ENG = ("pe", "act", "dve", "pool", "sp")

    def __init__(self, nc, stack):
        self.nc = nc
        self.stack = stack
        self.streams = {e: [] for e in self.ENG}
        self.sem = {}
        for e in ("pe", "act", "dve", "pool"):
            self.sem[e] = stack.enter_context(nc.semaphore("s_" + e))
        self.cnt = {e: 0 for e in self.ENG}
        self.dsem = {}
        self.dcnt = {}
        for q in ("sp", "act", "pool"):
            self.dsem[q] = [stack.enter_context(nc.semaphore("d_%s%d" % (q, i))) for i in range(NDMA_SEMS)]
            self.dcnt[q] = 0
        self.semobjs = {}
        self.known = {e: {} for e in self.ENG}
        self.lastw = {}
        self.reads = {}
        self.final_events = []
        self.ntile = 0

    def sb(self, shape, dtype=F32, name=None):
        self.ntile += 1
        return self.stack.enter_context(self.nc.sbuf_tensor(name or ("t%d" % self.ntile), list(shape), dtype))

    def ps(self, shape, dtype=F32, name=None):
        self.ntile += 1
        return self.stack.enter_context(self.nc.psum_tensor(name or ("p%d" % self.ntile), list(shape), dtype))

    @staticmethod
    def _k(k):
        if isinstance(k, tuple):
            return (id(k[0]),) + tuple(k[1:])
        return id(k)

    def _deps(self, eng, reads, writes):
        reads = [self._k(k) for k in reads]
        writes = [self._k(k) for k in writes]
        evs = []
        for k in reads:
            ev = self.lastw.get(k)
            if ev is not None:
                evs.append(ev)
        for k in writes:
            ev = self.lastw.get(k)
            if ev is not None:
                evs.append(ev)
            evs.extend(self.reads.get(k, ()))
        waits = []
        kn = self.known[eng]
        best = {}
        for (s, v, src) in evs:
            if src == "pe" and eng == "pe":
                continue
            if kn.get(s, 0) >= v:
                continue
            if best.get(s, (0,))[0] < v:
                best[s] = (v, src)
        for s, (v, src) in best.items():
            kn[s] = v
            waits.append((s, v))
        return waits

    def _commit(self, ev, reads, writes):
        reads = [self._k(k) for k in reads]
        writes = [self._k(k) for k in writes]
        for k in reads:
            self.reads.setdefault(k, []).append(ev)
        for k in writes:
            self.lastw[k] = ev
            self.reads[k] = []

    def op(self, eng, fn, reads=(), writes=()):
        waits = self._deps(eng, reads, writes)
        self.cnt[eng] += 1
        sid = "c_" + eng
        self.semobjs[sid] = self.sem[eng]
        ev = (sid, self.cnt[eng], eng)
        self.streams[eng].append((waits, fn, (self.sem[eng], 1)))
        self._commit(ev, reads, writes)
        return ev

    def dma(self, q, out, in_, reads=(), writes=(), final=False):
        waits = self._deps(q, reads, writes)
        i = self.dcnt[q]
        self.dcnt[q] += 1
        slot = i % NDMA_SEMS
        val = 16 * (i // NDMA_SEMS + 1)
        sid = "d_%s%d" % (q, slot)
        self.semobjs[sid] = self.dsem[q][slot]
        if val > 16:
            kn = self.known[q]
            if kn.get(sid, 0) < val - 16:
                kn[sid] = val - 16
                waits.append((sid, val - 16))
        ev = (sid, val, "dma_" + q)
        fn = lambda e, out=out, in_=in_: e.dma_start(out=out, in_=in_)
        self.streams[q].append((waits, fn, (self.dsem[q][slot], 16)))
        self._commit(ev, reads, writes)
        if final:
            self.final_events.append(ev)
        return ev

    def emit(self):
        nc = self.nc
        engmap = {"pe": "tensor", "act": "scalar", "dve": "vector", "pool": "gpsimd", "sp": "sync"}
        fw = []
        for (s, v, src) in self.final_events:
            fw.append((s, v))
        with nc.Block() as block:
            for e in self.ENG:
                stream = self.streams[e]
                extra = fw if e == "sp" else []

                def body(engine, stream=stream, extra=extra):
                    for waits, fn, inc in stream:
                        for (s, v) in waits:
                            engine.wait_ge(self.semobjs[s], v)
                        ins = fn(engine)
                        ins.then_inc(inc[0], inc[1])
                    for (s, v) in extra:
                        engine.wait_ge(self.semobjs[s], v)
                getattr(block, engmap[e])(body)

BF16 = mybir.dt.bfloat16

def mk_helpers(P):
    def ACT(out, in_, func, R, W, scale=1.0, bias=None):
        kw = {}
        if bias is not None:
            kw["bias"] = bias
        P.op("act", lambda e: e.activation(out=out, in_=in_, func=func, scale=scale, **kw), R, W)
    def TT(eng, out, a, b, op, R, W):
        P.op(eng, lambda e: e.tensor_tensor(out=out, in0=a, in1=b, op=op), R, W)
    def TS(eng, out, a, s1, s2, op0, op1, R, W):
        P.op(eng, lambda e: e.tensor_scalar(out=out, in0=a, scalar1=s1, scalar2=s2, op0=op0, op1=op1), R, W)
    def STT(eng, out, a, s, b, op0, op1, R, W):
        P.op("dve", lambda e: e.scalar_tensor_tensor(out=out, in0=a, scalar=s, in1=b, op0=op0, op1=op1), R, W)
    def MM(out, lhsT, rhs, start, stop, R, W):
        P.op("pe", lambda e: e.matmul(out, lhsT, rhs, start=start, stop=stop), R, W)
    def TR(out, in_, ident, R, W):
        P.op("pe", lambda e: e.transpose(out, in_, ident), R, W)
    def RECIP(eng, out, in_, R, W):
        P.op(eng, lambda e: e.reciprocal(out=out, in_=in_), R, W)
    def CP(eng, out, in_, R, W):
        P.op(eng, lambda e: e.tensor_copy(out=out, in_=in_), R, W)
    return ACT, TT, TS, STT, MM, TR, RECIP, CP

VA = {}
def _va():
    o = 0
    for name, n in [("gpre",16),("gmv",4),("gmo",4),("mu_r",8),("mu_k",8),("mu_v",8),("mu_wd",1),("mu_ad",1),("mu_gd",2),
                    ("w0",8),("a0",8),("kk",8),("ka",8),("rk",8),("cw0",4),("cw1",4),("cw2",4),("sco",4)]:
        VA[name] = o; o += n
    VA["_n"] = o
_va()

def tiles_of(n, maxn=512):
    k = -(-n // maxn)
    base = n // k
    assert base * k == n and base % 2 == 0
    return [(i*base, base) for i in range(k)]

def build_A(NTOK=1024):
    NC = NTOK + 2
    nc = bass.Bass("TRN2", target_bir_lowering=False)
    xT = nc.dram_tensor("xT", [16,128,NC], F32, kind="ExternalInput").ap()
    w_in = nc.dram_tensor("w_in", [16,128,6080], F32, kind="ExternalInput").ap()
    vecs = nc.dram_tensor("vecs", [128, VA["_n"]], F32, kind="ExternalInput").ap()
    w_up = nc.dram_tensor("w_up", [96,1024], F32, kind="ExternalInput").ap()
    a_up = nc.dram_tensor("a_up", [96,1024], F32, kind="ExternalInput").ap()
    g_up = nc.dram_tensor("g_up", [2,128,1024], F32, kind="ExternalInput").ap()
    wsT = nc.dram_tensor("wsT", [4,128,128], F32, kind="ExternalInput").ap()
    bs = nc.dram_tensor("bs", [1,512], F32, kind="ExternalInput").ap()
    consts = nc.dram_tensor("consts", [128,512], F32, kind="ExternalInput").ap()
    oA = nc.dram_tensor("oA", [72,128,NTOK], F32, kind="ExternalOutput").ap()
    w_in_r = w_in.rearrange("k p c -> p k c")
    VAL = slice(2, NC); PV1 = slice(1, NC-1); PV2 = slice(0, NC-2)
    TTL = tiles_of(NC)
    VTL = tiles_of(NTOK)
    with ExitStack() as st:
        P = Prog(nc, st)
        ACT, TT, TS, STT, MM, TR, RECIP, CP = mk_helpers(P)
        V = P.sb([128, VA["_n"]]); C = P.sb([128,512])
        P.dma("sp", V[:], vecs, writes=[V]); P.dma("sp", C[:], consts, writes=[C])
        ones = C[:, 0:128]; blk64 = C[:, 128:256]; ident = C[:, 256:384]; mincl = C[:, 384:512]
        def vc(name, i=0, n=128):
            return V[0:n, VA[name]+i:VA[name]+i+1]
        eps = P.sb([128,2])
        P.op("dve", lambda e: e.memset(eps[:, 0:1], 1e-6), (), [eps])
        P.op("dve", lambda e: e.memset(eps[:, 1:2], 1e-24), (), [eps])
        XR = [P.sb([128, NC], F32R) for _ in range(16)]
        X = [t[:].bitcast(F32) for t in XR]
        for kc in range(16):
            P.dma("pool", XR[kc][:], xT[kc], writes=[XR[kc]])
        T = [P.sb([128, NC]) for _ in range(22)]
        PS = [P.ps([128,512]) for _ in range(8)]
        psi = [0]
        def nps():
            psi[0] = (psi[0] + 1) % 5
            return PS[psi[0]]
        ssp = [PS[5], PS[6], PS[7]]
        for kc in range(16):
            sq = T[kc % 2]
            ACT(sq[:], X[kc], AF.Square, [XR[kc]], [sq])
            for ti, (c0, n) in enumerate(TTL):
                MM(ssp[ti][:, :n], ones, sq[:, c0:c0+n], kc == 0, kc == 15, [sq, C], [ssp[ti]])
        rstd = T[2]; sr = T[3]
        for ti, (c0, n) in enumerate(TTL):
            ACT(sr[:, c0:c0+n], ssp[ti][:, :n], AF.Sqrt, [ssp[ti], eps], [sr], scale=1.0/2048, bias=eps[:, 0:1])
        RECIP("dve", rstd[:], sr[:], [sr], [rstd])
        for kc in range(16):
            STT("dve", XR[kc][:], X[kc], vc("gpre", kc), rstd[:], ALU.mult, ALU.mult, [XR[kc], V, rstd], [XR[kc]])
        WG = [P.sb([128,16,128], F32R) for _ in range(2)]
        wgi = [0]
        def proj(c0, w, dst):
            wg = WG[wgi[0] % 2]; wgi[0] += 1
            for q in range(4):
                P.dma("pool", wg[:, 4*q:4*q+4, :w], w_in_r[:, 4*q:4*q+4, c0:c0+w], writes=[wg])
            for ti, (t0, n) in enumerate(TTL):
                ps = nps()
                for kc in range(16):
                    MM(ps[:w, :n], wg[:, kc, :w], XR[kc][:, t0:t0+n], kc == 0, kc == 15, [wg, XR[kc]], [ps])
                if ti % 2 == 0:
                    ACT(dst[:w, t0:t0+n], ps[:w, :n], AF.Copy, [ps], [dst])
                else:
                    CP("dve", dst[:w, t0:t0+n], ps[:w, :n], [ps], [dst])
        def grstd(src_ap, srcR, mat, inv, epsap, out, tmp1, tmp2, np_=128):
            ACT(tmp1[:np_, :NTOK], src_ap, AF.Square, srcR, [tmp1])
            for (c0, n) in VTL:
                ps = nps()
                MM(ps[:np_, :n], mat[0:np_, 0:np_], tmp1[:np_, c0:c0+n], True, True, [tmp1, C], [ps])
                ACT(tmp2[:np_, c0:c0+n], ps[:np_, :n], AF.Sqrt, [ps, eps], [tmp2], scale=inv, bias=epsap)
            RECIP("dve", out[:np_, :NTOK], tmp2[:np_, :NTOK], [tmp2], [out])
        def store(idx, src, q):
            P.dma(q, oA[idx], src[:, :NTOK], reads=[src], final=True)
        pw, pa_, pg0, pg1 = T[4], T[5], T[6], T[7]
        proj(4096, 96, pw); proj(4192, 96, pa_); proj(4288, 128, pg0); proj(4416, 128, pg1)
        twd = P.sb([96, NTOK], F32R); ads = P.sb([96, NTOK], F32R); sgd = [P.sb([128, NTOK], F32R) for _ in range(2)]
        WU = P.sb([96,1024], F32R); AU = P.sb([96,1024], F32R); GU = P.sb([128,2,1024], F32R)
        P.dma("pool", WU[:], w_up, writes=[WU]); P.dma("pool", AU[:], a_up, writes=[AU])
        P.dma("pool", GU[:, 0, :], g_up[0], writes=[GU]); P.dma("pool", GU[:, 1, :], g_up[1], writes=[GU])
        def tshift(eng, src, mu, dst_ap, dstW, tmp, np_=128):
            TT(eng, tmp[:np_, :NTOK], src[:np_, PV1], src[:np_, VAL], ALU.subtract, [src], [tmp])
            STT(eng, dst_ap, tmp[:np_, :NTOK], mu, src[:np_, VAL], ALU.mult, ALU.add, [tmp, src, V], dstW)
        t0_, t1_ = T[8], T[9]
        tshift("dve", pw, vc("mu_wd", 0, 96), t1_[:96, :NTOK], [t1_], t0_, 96)
        ACT(twd[:], t1_[:96, :NTOK], AF.Tanh, [t1_], [twd])
        tshift("dve", pa_, vc("mu_ad", 0, 96), ads[:], [ads], t0_, 96)
        for j, pg in enumerate((pg0, pg1)):
            tshift("dve", pg, vc("mu_gd", j), t1_[:, :NTOK], [t1_], t0_)
            ACT(sgd[j][:], t1_[:, :NTOK], AF.Sigmoid, [t1_], [sgd[j]])
        omka = P.sb([128, 8])
        TS("dve", omka[:], V[:, VA["ka"]:VA["ka"]+8], -1.0, 1.0, ALU.mult, ALU.add, [V], [omka])
        for i in range(8):
            pr, pk, pv = T[4], T[5], T[6]
            proj(1024 + 128*i, 128, pr); proj(2048 + 128*i, 128, pk); proj(3072 + 128*i, 128, pv)
            rs, ks, vs, tmp = T[7], T[10], T[11], T[12]
            tshift("dve", pr, vc("mu_r", i), rs[:, :NTOK], [rs], tmp)
            tshift("pool", pk, vc("mu_k", i), ks[:, :NTOK], [ks], T[13])
            tshift("dve", pv, vc("mu_v", i), vs[:, :NTOK], [vs], tmp)
            lw, ag, gg = T[14], T[15], T[16]
            for (c0, n) in VTL:
                ps = nps()
                MM(ps[:, :n], WU[:, 128*i:128*i+128], twd[:, c0:c0+n], True, True, [WU, twd], [ps])
                ACT(lw[:, c0:c0+n], ps[:, :n], AF.Sigmoid, [ps, V], [lw], bias=vc("w0", i))
                ps = nps()
                MM(ps[:, :n], AU[:, 128*i:128*i+128], ads[:, c0:c0+n], True, True, [AU, ads], [ps])
                ACT(ag[:, c0:c0+n], ps[:, :n], AF.Sigmoid, [ps, V], [ag], bias=vc("a0", i))
                ps = nps()
                MM(ps[:, :n], GU[:, 0, 128*i:128*i+128], sgd[0][:, c0:c0+n], True, False, [GU, sgd[0]], [ps])
                MM(ps[:, :n], GU[:, 1, 128*i:128*i+128], sgd[1][:, c0:c0+n], False, True, [GU, sgd[1]], [ps])
                CP("dve", gg[:, c0:c0+n], ps[:, :n], [ps], [gg])
            TS("pool", lw[:, :NTOK], lw[:, :NTOK], -0.6065306597126334, None, ALU.mult, ALU.bypass, [lw], [lw])
            kkt, kn = T[17], T[18]
            TS("pool", kkt[:, :NTOK], ks[:, :NTOK], vc("kk", i), None, ALU.mult, ALU.bypass, [ks, V], [kkt])
            ACT(T[19][:, :NTOK], kkt[:, :NTOK], AF.Square, [kkt], [T[19]])
            for (c0, n) in VTL:
                ps = nps()
                MM(ps[:, :n], blk64, T[19][:, c0:c0+n], True, True, [T[19], C], [ps])
                ACT(kn[:, c0:c0+n], ps[:, :n], AF.Sqrt, [ps], [kn])
            TS("dve", kn[:, :NTOK], kn[:, :NTOK], 1e-12, None, ALU.max, ALU.bypass, [kn], [kn])
            RECIP("dve", kn[:, :NTOK], kn[:, :NTOK], [kn], [kn])
            av, bv = T[19], T[20]
            TT("dve", kkt[:, :NTOK], kkt[:, :NTOK], kn[:, :NTOK], ALU.mult, [kkt, kn], [kkt])
            TS("pool", av[:, :NTOK], kkt[:, :NTOK], -1.0, None, ALU.mult, ALU.bypass, [kkt], [av])
            TT("dve", bv[:, :NTOK], kkt[:, :NTOK], ag[:, :NTOK], ALU.mult, [kkt, ag], [bv])
            km = T[21]
            TS("dve", km[:, :NTOK], ag[:, :NTOK], vc("ka", i), omka[:, i:i+1], ALU.mult, ALU.add, [ag, V, omka], [km])
            TT("pool", km[:, :NTOK], km[:, :NTOK], ks[:, :NTOK], ALU.mult, [km, ks], [km])
            bo = T[13]
            TT("dve", tmp[:, :NTOK], rs[:, :NTOK], km[:, :NTOK], ALU.mult, [rs, km], [tmp])
            TS("dve", tmp[:, :NTOK], tmp[:, :NTOK], vc("rk", i), None, ALU.mult, ALU.bypass, [tmp, V], [tmp])
            for (c0, n) in VTL:
                ps = nps()
                MM(ps[:, :n], blk64, tmp[:, c0:c0+n], True, True, [tmp, C], [ps])
                TT("dve", bo[:, c0:c0+n], ps[:, :n], vs[:, c0:c0+n], ALU.mult, [ps, vs], [bo])
            for qi, src in enumerate((rs, km, vs, av, bv, lw, gg, bo)):
                store(8 + 8*qi + i, src, "sp" if qi % 2 == 0 else "act")
        WS = P.sb([128, 4, 128]); BS = P.sb([1, 512])
        for g in range(4):
            P.dma("sp", WS[:, g, :], wsT[g], writes=[WS])
        P.dma("sp", BS[:], bs, writes=[BS])
        for g in range(4):
            TT("dve", WS[:, g, :], WS[:, g, :], mincl, ALU.mult, [WS, C], [WS])
        def gelu(src, dst, t1):
            ACT(t1[:, :NTOK], src[:, VAL], AF.Square, [src], [t1])
            TS("dve", t1[:, :NTOK], t1[:, :NTOK], 0.044715, 1.0, ALU.mult, ALU.add, [t1], [t1])
            TT("pool", t1[:, :NTOK], t1[:, :NTOK], src[:, VAL], ALU.mult, [t1, src], [t1])
            ACT(t1[:, :NTOK], t1[:, :NTOK], AF.Sigmoid, [t1], [t1], scale=1.5957691216057308)
            TT("dve", dst[:, :NTOK], t1[:, :NTOK], src[:, VAL], ALU.mult, [t1, src], [dst])
        for g in range(4):
            pu, pv_ = T[4], T[5]
            proj(128*g, 128, pu); proj(512 + 128*g, 128, pv_)
            zu, zv = T[6], T[7]
            gelu(pu, zu, T[8]); gelu(pv_, zv, T[9])
            rs_ = T[10]
            grstd(zv[:, :NTOK], [zv], ones, 1.0/128, eps[:, 0:1], rs_, T[11], T[12])
            vn = T[13]
            STT("dve", vn[:, :NTOK], zv[:, :NTOK], vc("gmv", g), rs_[:, :NTOK], ALU.mult, ALU.mult, [zv, V, rs_], [vn])
            ya = T[14]
            for c in range(NTOK // 128):
                cs = slice(128*c, 128*c+128)
                ps = nps()
                TR(ps[:, 0:128], vn[:, cs], ident, [vn, C], [ps])
                vT = T[15 + (c % 2)]
                CP("dve", vT[:, 0:128], ps[:, 0:128], [ps], [vT])
                ps2 = nps()
                MM(ps2[:, 0:128], vT[:, 0:128], WS[:, g, :], True, False, [vT, WS], [ps2])
                MM(ps2[:, 0:128], ones[0:1, :], BS[0:1, 128*g:128*g+128], False, True, [C, BS], [ps2])
                TT("dve", ya[:, cs], ps2[:, 0:128], zu[:, cs], ALU.mult, [ps2, zu], [ya])
            grstd(ya[:, :NTOK], [ya], ones, 1.0/128, eps[:, 0:1], rs_, T[11], T[12])
            yo = T[17]
            STT("dve", yo[:, :NTOK], ya[:, :NTOK], vc("gmo", g), rs_[:, :NTOK], ALU.mult, ALU.mult, [ya, V, rs_], [yo])
            store(g, yo, "sp")
        for i in range(4):
            pb_, pc_, ph_ = T[4], T[5], T[6]
            proj(4544 + 128*i, 128, pb_); proj(5056 + 128*i, 128, pc_); proj(5568 + 128*i, 128, ph_)
            z = T[7]
            TT("pool", z[:], pc_[:], ph_[:], ALU.mult, [pc_, ph_], [z])
            acc = T[8]
            TS("dve", acc[:, :NTOK], z[:, VAL], vc("cw2", i), None, ALU.mult, ALU.bypass, [z, V], [acc])
            STT("dve", acc[:, :NTOK], z[:, PV1], vc("cw1", i), acc[:, :NTOK], ALU.mult, ALU.add, [z, V, acc], [acc])
            STT("dve", acc[:, :NTOK], z[:, PV2], vc("cw0", i), acc[:, :NTOK], ALU.mult, ALU.add, [z, V, acc], [acc])
            yc = T[9]
            TT("pool", yc[:, :NTOK], acc[:, :NTOK], pb_[:, VAL], ALU.mult, [acc, pb_], [yc])
            rs_ = T[10]
            grstd(yc[:, :NTOK], [yc], blk64, 1.0/64, eps[:, 0:1], rs_, T[11], T[12])
            yo = T[13]
            STT("dve", yo[:, :NTOK], yc[:, :NTOK], vc("sco", i), rs_[:, :NTOK], ALU.mult, ALU.mult, [yc, V, rs_], [yo])
            store(4 + i, yo, "act")
        P.emit()
    return nc

def host_consts():
    c = np.zeros((128, 512), np.float32)
    c[:, 0:128] = 1.0
    c[0:64, 128:192] = 1.0; c[64:128, 192:256] = 1.0
    c[:, 256:384] = np.eye(128, dtype=np.float32)
    c[:, 384:512] = np.triu(np.ones((128,128), np.float32))
    return c

def colsT(v, n):
    return np.ascontiguousarray(v.reshape(n, 128).T)

def host_vecs_A(inp, l):
    V = np.zeros((128, VA["_n"]), np.float32)
    def put(name, arr, n):
        V[:, VA[name]:VA[name]+n] = colsT(arr, n)
    put("gpre", inp["norm_mix_pre"][l], 16)
    put("gmv", inp["gm_v_gain"][l], 4); put("gmo", inp["gm_out_gain"][l], 4)
    mu = inp["rk_mu"][l]
    put("mu_r", mu[0:1024], 8); put("mu_k", mu[1024:2048], 8); put("mu_v", mu[2048:3072], 8)
    V[0:96, VA["mu_wd"]] = mu[3072:3168]; V[0:96, VA["mu_ad"]] = mu[3168:3264]
    put("mu_gd", mu[3264:3520], 2)
    put("w0", inp["rk_w0"][l], 8); put("a0", inp["rk_a0"][l], 8); put("kk", inp["rk_k_k"][l], 8); put("ka", inp["rk_k_a"][l], 8)
    put("rk", inp["rk_r_k"][l].reshape(-1), 8)
    for k in range(3):
        put("cw%d" % k, inp["sc_conv"][l][k], 4)
    put("sco", inp["sc_out_gain"][l], 4)
    return V


def build_B(SEQ=4096, NH=4):
    SEG = 512; NSEG = SEQ // SEG; CH = 64; NCH = SEG // CH
    nc = bass.Bass("TRN2", target_bir_lowering=False)
    sB = nc.dram_tensor("sB", [6, NH, 64, SEQ], F32, kind="ExternalInput").ap()
    cB = nc.dram_tensor("cB", [64, 64 + 320 + SEG], F32, kind="ExternalInput").ap()
    yB = nc.dram_tensor("yB", [NH, 64, SEQ], F32, kind="ExternalOutput").ap()
    with ExitStack() as st:
        P = Prog(nc, st)
        ACT, TT, TS, STT, MM, TR, RECIP, CP = mk_helpers(P)
        C = P.sb([64, 64 + 320 + SEG])
        P.dma("sp", C[:], cB, writes=[C])
        ident = C[:, 0:64]; mask5 = C[:, 64:384]; rmask = C[:, 384:384+SEG]
        PS = [P.ps([64, 512]) for _ in range(8)]
        hd = []
        for h in range(NH):
            d = {}
            for nm in ("r","k","v","a","b","lw","G","e1","e2","at","bt","kt","rt","bh","kh","Y"):
                d[nm] = P.sb([64, SEG])
            d["H"] = [P.sb([64,64]) for _ in range(2)]
            d["TM"] = P.sb([64,256]); d["AM"] = P.sb([64,320]); d["Z"] = [P.sb([64,128]) for _ in range(2)]
            d["PP"] = [P.sb([64,128]) for _ in range(2)]; d["AbT"] = P.sb([64,64]); d["U"] = P.sb([64,64])
            hd.append(d)
            P.op("pool", lambda e, t=d["H"][0]: e.memset(t[:], 0.0), (), [d["H"][0]])
        hcur = [0]*NH
        for seg in range(NSEG):
            s0 = seg * SEG
            for h in range(NH):
                d = hd[h]
                for qi, nm in enumerate(("r","k","v","a","b","lw")):
                    P.dma("sp" if qi % 2 == 0 else "act", d[nm][:], sB[qi, h, :, s0:s0+SEG], writes=[d[nm]])
                P.op("dve", lambda e, d=d: e.tensor_tensor_scan(out=d["G"][:], data0=rmask, data1=d["lw"][:], initial=0.0, op0=ALU.mult, op1=ALU.add), [d["lw"], C], [d["G"]])
                ACT(d["e1"][:], d["G"][:], AF.Exp, [d["G"]], [d["e1"]])
                TT("pool", d["rt"][:], d["r"][:], d["e1"][:], ALU.mult, [d["r"], d["e1"]], [d["rt"]])
                TT("pool", d["e2"][:], d["G"][:], d["lw"][:], ALU.subtract, [d["G"], d["lw"]], [d["e2"]])
                ACT(d["e2"][:], d["e2"][:], AF.Exp, [d["e2"]], [d["e2"]])
                TT("pool", d["at"][:], d["a"][:], d["e2"][:], ALU.mult, [d["a"], d["e2"]], [d["at"]])
                ACT(d["e2"][:], d["G"][:], AF.Exp, [d["G"], d["at"]], [d["e2"]], scale=-1.0)
                TT("pool", d["bt"][:], d["b"][:], d["e2"][:], ALU.mult, [d["b"], d["e2"]], [d["bt"]])
                TT("dve", d["kt"][:], d["k"][:], d["e2"][:], ALU.mult, [d["k"], d["e2"]], [d["kt"]])
                for c in range(NCH):
                    cs = slice(c*CH, c*CH+CH)
                    ACT(d["e2"][:, cs], d["G"][:, cs], AF.Exp, [d["G"], d["bt"], d["kt"]], [d["e2"]], scale=-1.0, bias=d["G"][:, c*CH+CH-1:c*CH+CH])
                TT("pool", d["bh"][:], d["b"][:], d["e2"][:], ALU.mult, [d["b"], d["e2"]], [d["bh"]])
                TT("dve", d["kh"][:], d["k"][:], d["e2"][:], ALU.mult, [d["k"], d["e2"]], [d["kh"]])
            for c in range(NCH):
                cs = slice(c*CH, c*CH+CH)
                for h in range(NH):
                    d = hd[h]
                    pT, pA, pZ, pQ, pU, pY, pH, pZ2 = PS
                    for j, nm in enumerate(("v","kh","bh","at")):
                        TR(pT[:, 64*j:64*j+64], d[nm][:, cs], ident, [d[nm], C], [pT])
                    CP("dve", d["TM"][:], pT[:, 0:256], [pT], [d["TM"]])
                    Vt = d["TM"][:, 0:64]; KHt = d["TM"][:, 64:128]; BHt = d["TM"][:, 128:192]; ATt = d["TM"][:, 192:256]
                    at, bt, kt, rt = d["at"][:, cs], d["bt"][:, cs], d["kt"][:, cs], d["rt"][:, cs]
                    for j, (l_, r_) in enumerate(((at, bt), (bt, at), (kt, at), (bt, rt), (kt, rt))):
                        MM(pA[:, 64*j:64*j+64], l_, r_, True, True, [d["at"], d["bt"], d["kt"], d["rt"]], [pA])
                    TT("dve", d["AM"][:], pA[:, 0:320], mask5, ALU.mult, [pA, C], [d["AM"]])
                    AM = d["AM"]
                    AakT = AM[:, 128:192]; ArbT = AM[:, 192:256]; ArkT = AM[:, 256:320]
                    MM(pZ[:, 0:64], AakT, Vt, True, True, [AM, d["TM"]], [pZ])
                    Z = d["Z"]
                    CP("pool", Z[0][:, 0:64], ATt, [d["TM"]], [Z[0]])
                    ACT(Z[0][:, 64:128], pZ[:, 0:64], AF.Copy, [pZ], [Z[0]])
                    zi = 0
                    Pk = AM[:, 0:64]; PkT = AM[:, 64:128]; Pkey = AM
                    for k in range(6):
                        pz = pZ if k % 2 == 0 else pZ2
                        MM(pz[:, 0:128], PkT, Z[zi][:], True, True, [Pkey, Z[zi]], [pz])
                        TT("dve", Z[1-zi][:], Z[zi][:], pz[:, 0:128], ALU.add, [Z[zi], pz], [Z[1-zi]])
                        zi = 1 - zi
                        if k < 5:
                            MM(pQ[:, 0:64], PkT, Pk, True, True, [Pkey], [pQ])
                            MM(pQ[:, 64:128], Pk, PkT, True, True, [Pkey], [pQ])
                            nxt = d["PP"][k % 2]
                            ACT(nxt[:], pQ[:, 0:128], AF.Copy, [pQ], [nxt])
                            Pk = nxt[:, 0:64]; PkT = nxt[:, 64:128]; Pkey = nxt
                    Zf = Z[zi]
                    TR(pU[:, 0:64], Zf[:, 0:64], ident, [Zf, C], [pU])
                    ACT(d["AbT"][:], pU[:, 0:64], AF.Copy, [pU], [d["AbT"]])
                    H = d["H"][hcur[h]]; Hn = d["H"][1 - hcur[h]]
                    MM(pU[:, 64:128], d["AbT"][:], H[:], True, True, [d["AbT"], H], [pU])
                    TT("dve", d["U"][:], pU[:, 64:128], Zf[:, 64:128], ALU.add, [pU, Zf], [d["U"]])
                    MM(pY[:, 0:64], H[:], rt, True, False, [H, d["rt"]], [pY])
                    MM(pY[:, 0:64], d["U"][:], ArbT, False, False, [d["U"], AM], [pY])
                    MM(pY[:, 0:64], Vt, ArkT, False, True, [d["TM"], AM], [pY])
                    ACT(d["Y"][:, cs], pY[:, 0:64], AF.Copy, [pY], [d["Y"]])
                    MM(pH[:, 0:64], BHt, d["U"][:], True, False, [d["TM"], d["U"]], [pH])
                    MM(pH[:, 0:64], KHt, Vt, False, True, [d["TM"]], [pH])
                    STT("dve", Hn[:], H[:], d["e1"][:, c*CH+CH-1:c*CH+CH], pH[:, 0:64], ALU.mult, ALU.add, [H, d["e1"], pH], [Hn])
                    hcur[h] = 1 - hcur[h]
            for h in range(NH):
                P.dma("sp", yB[h, :, s0:s0+SEG], hd[h]["Y"][:], reads=[hd[h]["Y"]], final=True)
        P.emit()
    return nc

def host_consts_B(SEG=512):
    c = np.zeros((64, 64 + 320 + SEG), np.float32)
    c[:, 0:64] = np.eye(64)
    sl_ts = np.tril(np.ones((64,64)), -1)
    st_T = np.triu(np.ones((64,64)), 1)
    in_T = np.triu(np.ones((64,64)), 0)
    c[:, 64:128] = sl_ts; c[:, 128:192] = st_T; c[:, 192:256] = st_T; c[:, 256:320] = in_T; c[:, 320:384] = in_T
    rm = np.ones(SEG, np.float32); rm[::64] = 0.0
    c[:, 384:] = rm[None, :]
    return c


VC = {}
def _vcl():
    o = 0
    for name, n in [("gpost",16),("gpre2",16),("gpost2",16),("lnw",8),("lnb",8)]:
        VC[name] = o; o += n
    VC["_n"] = o
_vcl()

def build_C(NTOK=1024):
    NT = 512; NHALF = NTOK // NT
    nc = bass.Bass("TRN2", target_bir_lowering=False)
    xT = nc.dram_tensor("xT", [16,128,NTOK], F32, kind="ExternalInput").ap()
    mA = nc.dram_tensor("mA", [8,128,NTOK], F32, kind="ExternalInput").ap()
    yR = nc.dram_tensor("yR", [8,128,NTOK], F32, kind="ExternalInput").ap()
    gb = nc.dram_tensor("gb", [16,128,NTOK], F32, kind="ExternalInput").ap()
    w_out = nc.dram_tensor("w_out", [16,128,2048], F32, kind="ExternalInput").ap()
    m_up = nc.dram_tensor("m_up", [16,128,8192], F32, kind="ExternalInput").ap()
    m_dn = nc.dram_tensor("m_dn", [64,128,2048], F32, kind="ExternalInput").ap()
    vecs = nc.dram_tensor("vecs", [128, VC["_n"]], F32, kind="ExternalInput").ap()
    consts = nc.dram_tensor("consts", [128,512], F32, kind="ExternalInput").ap()
    oC = nc.dram_tensor("oC", [16,128,NTOK], F32, kind="ExternalOutput").ap()
    w_out_r = w_out.rearrange("k p c -> p k c"); m_up_r = m_up.rearrange("k p c -> p k c"); m_dn_r = m_dn.rearrange("k p c -> p k c")
    with ExitStack() as st:
        P = Prog(nc, st)
        ACT, TT, TS, STT, MM, TR, RECIP, CP = mk_helpers(P)
        V = P.sb([128, VC["_n"]]); C = P.sb([128,512])
        P.dma("sp", V[:], vecs, writes=[V]); P.dma("sp", C[:], consts, writes=[C])
        ones = C[:, 0:128]; blk64 = C[:, 128:256]
        def vc(name, i=0):
            return V[:, VC[name]+i:VC[name]+i+1]
        eps = P.sb([128,2])
        P.op("dve", lambda e: e.memset(eps[:, 0:1], 1e-6), (), [eps])
        P.op("dve", lambda e: e.memset(eps[:, 1:2], 64e-5), (), [eps])
        CAT = [P.sb([128, NT], F32R) for _ in range(16)]
        Y2 = [P.sb([128, NT]) for _ in range(16)]
        X1 = [P.sb([128, NT]) for _ in range(16)]
        F = [P.sb([128, NT], BF16) for _ in range(64)]
        WG = [P.sb([128,16,128], F32R) for _ in range(2)]
        WD = [P.sb([128,16,128], BF16) for _ in range(2)]
        T = [P.sb([128, NT]) for _ in range(6)]
        PS = [P.ps([128,512]) for _ in range(8)]
        psi = [0]
        def nps():
            psi[0] = (psi[0] + 1) % 6
            return PS[psi[0]]
        pss = PS[7]
        wgi = [0]; wdi = [0]
        def rms_all(src, out):
            for kc in range(16):
                sq = T[4 + kc % 2]
                ACT(sq[:], src[kc][:], AF.Square, [src[kc]], [sq])
                MM(pss[:, :NT], ones, sq[:], kc == 0, kc == 15, [sq, C], [pss])
            ACT(out[:], pss[:, :NT], AF.Sqrt, [pss, eps], [out], scale=1.0/2048, bias=eps[:, 0:1])
            RECIP("dve", out[:], out[:], [out], [out])
        def big_proj(wr, c0, rhs_tiles, evac):
            wg = WG[wgi[0] % 2]; wgi[0] += 1
            for q in range(4):
                P.dma("pool", wg[:, 4*q:4*q+4, :], wr[:, 4*q:4*q+4, c0:c0+128], writes=[wg])
            ps = nps()
            for kc in range(16):
                MM(ps[:, :NT], wg[:, kc, :], rhs_tiles[kc][:], kc == 0, kc == 15, [wg, rhs_tiles[kc]], [ps])
            evac(ps)
        for hf in range(NHALF):
            hs = slice(hf*NT, hf*NT+NT)
            for j in range(4):
                P.dma("pool", CAT[j][:], mA[j][:, hs], writes=[CAT[j]])
                P.dma("pool", CAT[12+j][:], mA[4+j][:, hs], writes=[CAT[12+j]])
            for kc in range(16):
                P.dma("sp" if kc % 2 == 0 else "act", X1[kc][:], xT[kc][:, hs], writes=[X1[kc]])
            for i in range(8):
                y, g_, bo, t3 = T[0], T[1], T[2], T[3]
                P.dma("sp", y[:], yR[i][:, hs], writes=[y])
                P.dma("act", g_[:], gb[i][:, hs], writes=[g_])
                P.dma("sp", bo[:], gb[8+i][:, hs], writes=[bo])
                ps1 = nps()
                MM(ps1[:, :NT], blk64, y[:], True, True, [y, C], [ps1])
                ACT(t3[:], y[:], AF.Square, [y], [t3])
                ps2 = nps()
                MM(ps2[:, :NT], blk64, t3[:], True, True, [t3, C], [ps2])
                mean = T[4]
                ACT(mean[:], ps1[:, :NT], AF.Copy, [ps1], [mean], scale=1.0/64)
                TT("pool", t3[:], mean[:], mean[:], ALU.mult, [mean], [t3])
                STT("dve", t3[:], ps2[:, :NT], 1.0/64, t3[:], ALU.mult, ALU.subtract, [ps2, t3], [t3])
                ACT(t3[:], t3[:], AF.Sqrt, [t3, eps], [t3], bias=eps[:, 1:2])
                RECIP("dve", t3[:], t3[:], [t3], [t3])
                TT("pool", y[:], y[:], mean[:], ALU.subtract, [y, mean], [y])
                TT("dve", y[:], y[:], t3[:], ALU.mult, [y, t3], [y])
                TS("dve", y[:], y[:], vc("lnw", i), vc("lnb", i), ALU.mult, ALU.add, [y, V], [y])
                TT("pool", y[:], y[:], bo[:], ALU.add, [y, bo], [y])
                TT("dve", CAT[4+i][:], y[:], g_[:], ALU.mult, [y, g_], [CAT[4+i]])
            for oc in range(16):
                def ev(ps, oc=oc):
                    if oc % 2 == 0:
                        ACT(Y2[oc][:], ps[:, :NT], AF.Copy, [ps], [Y2[oc]])
                    else:
                        CP("dve", Y2[oc][:], ps[:, :NT], [ps], [Y2[oc]])
                big_proj(w_out_r, oc*128, CAT, ev)
            R = T[0]
            rms_all(Y2, R)
            for oc in range(16):
                t = T[1 + oc % 2]
                STT("dve", t[:], Y2[oc][:], vc("gpost", oc), R[:], ALU.mult, ALU.mult, [Y2[oc], V, R], [t])
                TT("pool", X1[oc][:], X1[oc][:], t[:], ALU.add, [X1[oc], t], [X1[oc]])
            rms_all(X1, R)
            for kc in range(16):
                STT("dve", CAT[kc][:], X1[kc][:], vc("gpre2", kc), R[:], ALU.mult, ALU.mult, [X1[kc], V, R], [CAT[kc]])
            for hc in range(64):
                def ev(ps, hc=hc):
                    t = T[1 + hc % 2]
                    ACT(t[:], ps[:, :NT], AF.Relu, [ps], [t])
                    TT("dve" if hc % 2 == 0 else "pool", F[hc][:], t[:], t[:], ALU.mult, [t], [F[hc]])
                big_proj(m_up_r, hc*128, CAT, ev)
            for oc in range(16):
                ps = nps()
                for pc in range(4):
                    wd = WD[wdi[0] % 2]; wdi[0] += 1
                    for q in range(2):
                        P.dma("pool", wd[:, 8*q:8*q+8, :], m_dn_r[:, 16*pc+8*q:16*pc+8*q+8, oc*128:oc*128+128], writes=[wd])
                    for j in range(16):
                        hc = 16*pc + j
                        MM(ps[:, :NT], wd[:, j, :], F[hc][:], hc == 0, hc == 63, [wd, F[hc]], [ps])
                if oc % 2 == 0:
                    ACT(Y2[oc][:], ps[:, :NT], AF.Copy, [ps], [Y2[oc]])
                else:
                    CP("dve", Y2[oc][:], ps[:, :NT], [ps], [Y2[oc]])
            rms_all(Y2, R)
            for oc in range(16):
                t = T[1 + oc % 2]
                STT("dve", t[:], Y2[oc][:], vc("gpost2", oc), R[:], ALU.mult, ALU.mult, [Y2[oc], V, R], [t])
                TT("pool", t[:], X1[oc][:], t[:], ALU.add, [X1[oc], t], [t])
                P.dma("sp" if oc % 2 == 0 else "act", oC[oc][:, hs], t[:], reads=[t], final=True)
        P.emit()
    return nc

def host_vecs_C(inp, l):
    V = np.zeros((128, VC["_n"]), np.float32)
    def put(name, arr, n):
        V[:, VC[name]:VC[name]+n] = colsT(arr, n)
    put("gpost", inp["norm_mix_post"][l], 16); put("gpre2", inp["norm_mlp_pre"][l], 16); put("gpost2", inp["norm_mlp_post"][l], 16)
    put("lnw", inp["rk_ln_w"][l], 8); put("lnb", inp["rk_ln_b"][l], 8)
    return V


_PROGS = {}
def _prog(name):
    if name not in _PROGS:
        _PROGS[name] = {"A": build_A, "B": build_B, "C": build_C}[name]()
    return _PROGS[name]

def kernel(**inp):
    inp = {k: np.asarray(v, dtype=np.float32) for k, v in inp.items()}
    x = inp["x"]
    L = inp["w_in"].shape[0]
    cA = host_consts(); cBc = host_consts_B()
    cores = list(range(8))
    for l in range(L):
        w_in = inp["w_in"][l].reshape(16, 128, 6080)
        vA = host_vecs_A(inp, l)
        g_up = inp["rk_g_up"][l].reshape(2, 128, 1024)
        wsT = np.ascontiguousarray(inp["gm_ws"][l].transpose(0, 2, 1))
        bs = inp["gm_bs"][l].reshape(1, 512)
        maps = []
        for c in cores:
            b, q = c // 4, c % 4
            xT = np.zeros((16, 128, 1026), np.float32)
            xT[:, :, 2:] = x[b, q*1024:(q+1)*1024].T.reshape(16, 128, 1024)
            if q > 0:
                xT[:, :, 0:2] = x[b, q*1024-2:q*1024].T.reshape(16, 128, 2)
            maps.append(dict(xT=xT, w_in=w_in, vecs=vA, w_up=inp["rk_w_up"][l], a_up=inp["rk_a_up"][l], g_up=g_up, wsT=wsT, bs=bs, consts=cA))
        resA = run_bass_kernel_spmd(_prog("A"), maps, core_ids=cores)
        oA = [np.asarray(r["oA"]) for r in resA.results]
        maps = []
        for c in cores:
            b, hq = c // 4, c % 4
            sB = np.zeros((6, 4, 64, 4096), np.float32)
            for qi in range(6):
                for q in range(4):
                    blk = oA[b*4+q][8+8*qi:16+8*qi].reshape(1024, 1024)
                    sB[qi, :, :, q*1024:(q+1)*1024] = blk[256*hq:256*hq+256].reshape(4, 64, 1024)
            maps.append(dict(sB=sB, cB=cBc))
        resB = run_bass_kernel_spmd(_prog("B"), maps, core_ids=cores)
        yB = [np.asarray(r["yB"]) for r in resB.results]
        w_out = inp["w_out"][l].reshape(16, 128, 2048)
        m_up = inp["mlp_up"][l].reshape(16, 128, 8192)
        m_dn = inp["mlp_down"][l].reshape(64, 128, 2048)
        vCc = host_vecs_C(inp, l)
        maps = []
        for c in cores:
            b, q = c // 4, c % 4
            xT = np.ascontiguousarray(x[b, q*1024:(q+1)*1024].T).reshape(16, 128, 1024)
            yR = np.concatenate([yB[b*4+hq][:, :, q*1024:(q+1)*1024].reshape(256, 1024) for hq in range(4)], 0).reshape(8, 128, 1024)
            maps.append(dict(xT=xT, mA=np.ascontiguousarray(oA[c][0:8]), yR=np.ascontiguousarray(yR), gb=np.ascontiguousarray(oA[c][56:72]),
                             w_out=w_out, m_up=m_up, m_dn=m_dn, vecs=vCc, consts=cA))
        resC = run_bass_kernel_spmd(_prog("C"), maps, core_ids=cores)
        xn = np.zeros_like(x)
        for c in cores:
            b, q = c // 4, c % 4
            xn[b, q*1024:(q+1)*1024] = np.asarray(resC.results[c]["oC"]).reshape(2048, 1024).T
        x = xn
    return x.astype(np.float32)
```
